# Optimizing a Trainium2 kernel written in Bass

```python
import math
import jax, jax.numpy as jnp
from jax import lax
import numpy as np

D_MODEL = 2048
BATCH = 2
SEQ = 8192
DEPTH = 4

CHUNK = 64
Q_BLOCK = 128
EPS = 1e-6
NEG_INF = -1e30

FOX_HEADS = 8
FOX_HEAD_DIM = D_MODEL // 16
FOX_WIDTH = FOX_HEADS * FOX_HEAD_DIM
SSM_WIDTH = D_MODEL // 2
SSM_HEAD_DIM = 64
SSM_HEADS = SSM_WIDTH // SSM_HEAD_DIM
SSM_GROUPS = 2
SSM_STATE = 128
SSM_CONV = 4
SSM_CONV_DIM = SSM_WIDTH + 2 * SSM_GROUPS * SSM_STATE
SC_WIDTH = D_MODEL // 2
SC_CONV = 3
CF_WIDTH = D_MODEL // 2
CF_CONV = 31
N_BRANCH = 4
BRANCH_WIDTH = D_MODEL // 2

IN_SIZES = (
    FOX_WIDTH, FOX_WIDTH, FOX_WIDTH, FOX_HEADS, FOX_WIDTH,
    SSM_WIDTH, SSM_CONV_DIM, SSM_HEADS,
    SC_WIDTH, SC_WIDTH, SC_WIDTH, SC_WIDTH,
    2 * CF_WIDTH, CF_WIDTH,
)
N_IN = sum(IN_SIZES)

kernel_name = "hybrid_fox_ssd_shortconv_conformer_block"


def _split_columns(u, sizes):
    parts, off = [], 0
    for s in sizes:
        parts.append(u[..., off:off + s])
        off += s
    return parts


def rms_norm(x, w):
    xf = x.astype(jnp.float32)
    y = xf * lax.rsqrt(jnp.mean(xf * xf, axis=-1, keepdims=True) + EPS)
    return (y * w.astype(jnp.float32)).astype(x.dtype)


def layer_norm(x, w, b):
    xf = x.astype(jnp.float32)
    mu = jnp.mean(xf, axis=-1, keepdims=True)
    var = jnp.mean(jnp.square(xf - mu), axis=-1, keepdims=True)
    return ((xf - mu) * lax.rsqrt(var + EPS) * w + b).astype(x.dtype)


def causal_dwconv(x, w, b):
    k = w.shape[0]
    y = lax.conv_general_dilated(
        x, w[:, None, :].astype(x.dtype), window_strides=(1,), padding=((k - 1, 0),),
        dimension_numbers=("NWC", "WIO", "NWC"), feature_group_count=x.shape[-1])
    return y + b.astype(x.dtype)


def forgetting_attention(q, k, v, logf):
    bsz, s_len, n_h, dh = q.shape
    nb = s_len // Q_BLOCK
    scale = dh ** -0.5
    c = jnp.cumsum(logf, axis=1)
    qb = q.reshape(bsz, nb, Q_BLOCK, n_h, dh).transpose(1, 0, 3, 2, 4)
    cq = c.reshape(bsz, nb, Q_BLOCK, n_h).transpose(1, 0, 3, 2)
    kt = k.transpose(0, 2, 1, 3)
    vt = v.transpose(0, 2, 1, 3)
    ck = c.transpose(0, 2, 1)
    kpos = jnp.arange(s_len)

    def block(args):
        q_i, c_i, i = args
        s = jnp.einsum("bhqd,bhkd->bhqk", q_i, kt,
                       preferred_element_type=jnp.float32) * scale
        s = s + c_i[..., None] - ck[:, :, None, :]
        qpos = i * Q_BLOCK + jnp.arange(Q_BLOCK)
        s = jnp.where(kpos[None, :] <= qpos[:, None], s, NEG_INF)
        p = jax.nn.softmax(s, axis=-1)
        return jnp.einsum("bhqk,bhkd->bhqd", p.astype(vt.dtype), vt)

    o = lax.map(block, (qb, cq, jnp.arange(nb)))
    return o.transpose(1, 0, 3, 2, 4).reshape(bsz, s_len, n_h * dh)


def ssd(xh, dt, a, bm, cm, d_skip):
    bsz, s_len, n_h, p_dim = xh.shape
    g, n = bm.shape[2], bm.shape[3]
    r = n_h // g
    nc = s_len // CHUNK
    x = (xh * dt[..., None]).reshape(bsz, nc, CHUNK, g, r, p_dim)
    da = (dt * a).reshape(bsz, nc, CHUNK, g, r)
    bc = bm.reshape(bsz, nc, CHUNK, g, n)
    cc = cm.reshape(bsz, nc, CHUNK, g, n)
    cs = jnp.cumsum(da, axis=2)
    seg = cs[:, :, :, None] - cs[:, :, None, :]
    causal = jnp.tril(jnp.ones((CHUNK, CHUNK), dtype=bool))
    lmat = jnp.exp(jnp.where(causal[:, :, None, None], seg, -jnp.inf))
    cb = jnp.einsum("bclgn,bcsgn->bclsg", cc, bc)
    y_diag = jnp.einsum("bclsg,bclsgr,bcsgrp->bclgrp", cb, lmat, x)
    decay = jnp.exp(cs[:, :, -1:] - cs)
    states = jnp.einsum("bclgn,bclgr,bclgrp->bcgrpn", bc, decay, x)
    chunk_decay = jnp.exp(cs[:, :, -1])

    def step(h, inp):
        s_c, a_c = inp
        return h * a_c[..., None, None] + s_c, h

    h0 = jnp.zeros((bsz, g, r, p_dim, n), dtype=states.dtype)
    _, prev = lax.scan(step, h0, (jnp.moveaxis(states, 1, 0),
                                  jnp.moveaxis(chunk_decay, 1, 0).astype(states.dtype)))
    prev = jnp.moveaxis(prev, 0, 1)
    y_off = jnp.einsum("bclgn,bcgrpn,bclgr->bclgrp", cc, prev, jnp.exp(cs))
    y = (y_diag + y_off).reshape(bsz, s_len, n_h, p_dim) + xh * d_skip[:, None]
    return y.reshape(bsz, s_len, n_h * p_dim).astype(xh.dtype)


def hybrid_layer(x, norm_w, w_in, fg_bias, ssm_conv_w, ssm_conv_b, dt_bias, a_log, d_skip,
                 ssm_norm_w, sc_conv_w, sc_conv_b, cf_conv_w, cf_conv_b, cf_ln_w, cf_ln_b,
                 w_gate, b_gate, w_branch, w_out):
    bsz, s_len, _ = x.shape
    h = rms_norm(x, norm_w)
    u = h @ w_in
    (q, k, v, f_raw, g_a, z, xbc, dt_raw,
     sc_b, sc_c, sc_x, g_c, glu, g_d) = _split_columns(u, IN_SIZES)

    logf = jax.nn.log_sigmoid((f_raw + fg_bias).astype(jnp.float32))
    heads = lambda t: t.reshape(bsz, s_len, FOX_HEADS, FOX_HEAD_DIM)
    y_a = forgetting_attention(heads(q), heads(k), heads(v), logf) * jax.nn.silu(g_a)

    xbc = jax.nn.silu(causal_dwconv(xbc, ssm_conv_w, ssm_conv_b))
    xs, bm, cm = _split_columns(xbc, (SSM_WIDTH, SSM_GROUPS * SSM_STATE, SSM_GROUPS * SSM_STATE))
    dt = jax.nn.softplus((dt_raw + dt_bias).astype(jnp.float32))
    a = -jnp.exp(a_log.astype(jnp.float32))
    y = ssd(xs.reshape(bsz, s_len, SSM_HEADS, SSM_HEAD_DIM), dt, a,
            bm.reshape(bsz, s_len, SSM_GROUPS, SSM_STATE),
            cm.reshape(bsz, s_len, SSM_GROUPS, SSM_STATE), d_skip)
    y_b = rms_norm(y * jax.nn.silu(z), ssm_norm_w)

    y_c = sc_b * causal_dwconv(sc_c * sc_x, sc_conv_w, sc_conv_b) * jax.nn.silu(g_c)

    glu_a, glu_g = _split_columns(glu, (CF_WIDTH, CF_WIDTH))
    cf = causal_dwconv(glu_a * jax.nn.sigmoid(glu_g), cf_conv_w, cf_conv_b)
    y_d = jax.nn.silu(layer_norm(cf, cf_ln_w, cf_ln_b)) * jax.nn.silu(g_d)

    merged = None
    for i, y_i in enumerate((y_a, y_b, y_c, y_d)):
        gate = jax.nn.sigmoid(h @ w_gate[i] + b_gate[i])
        term = gate * (y_i @ w_branch[i])
        merged = term if merged is None else merged + term
    return x + merged @ w_out


def setup_inputs(seed: int = 0) -> dict:
    key = jax.random.key(seed)
    ks = jax.random.split(key, 24)
    f32 = jnp.float32
    L = DEPTH

    def nrm(k, shape, scale):
        return jax.random.normal(k, shape, f32) * scale

    dt0 = jnp.exp(jax.random.uniform(ks[6], (L, SSM_HEADS), f32, math.log(1e-3), math.log(1e-1)))
    return {
        "x": nrm(ks[0], (BATCH, SEQ, D_MODEL), 1.0),
        "norm_w": 1.0 + nrm(ks[1], (L, D_MODEL), 0.02),
        "w_in": nrm(ks[2], (L, D_MODEL, N_IN), D_MODEL ** -0.5),
        "fg_bias": jax.random.uniform(ks[3], (L, FOX_HEADS), f32, 1.0, 6.0),
        "ssm_conv_w": nrm(ks[4], (L, SSM_CONV, SSM_CONV_DIM), SSM_CONV ** -0.5),
        "ssm_conv_b": nrm(ks[5], (L, SSM_CONV_DIM), 0.02),
        "dt_bias": dt0 + jnp.log(-jnp.expm1(-dt0)),
        "a_log": jnp.log(jax.random.uniform(ks[7], (L, SSM_HEADS), f32, 1.0, 16.0)),
        "d_skip": 1.0 + nrm(ks[8], (L, SSM_HEADS), 0.02),
        "ssm_norm_w": 1.0 + nrm(ks[9], (L, SSM_WIDTH), 0.02),
        "sc_conv_w": nrm(ks[10], (L, SC_CONV, SC_WIDTH), SC_CONV ** -0.5),
        "sc_conv_b": nrm(ks[11], (L, SC_WIDTH), 0.02),
        "cf_conv_w": nrm(ks[12], (L, CF_CONV, CF_WIDTH), CF_CONV ** -0.5),
        "cf_conv_b": nrm(ks[13], (L, CF_WIDTH), 0.02),
        "cf_ln_w": 1.0 + nrm(ks[14], (L, CF_WIDTH), 0.02),
        "cf_ln_b": nrm(ks[15], (L, CF_WIDTH), 0.02),
        "w_gate": nrm(ks[16], (L, N_BRANCH, D_MODEL, D_MODEL), D_MODEL ** -0.5),
        "b_gate": nrm(ks[17], (L, N_BRANCH, D_MODEL), 0.02),
        "w_branch": nrm(ks[18], (L, N_BRANCH, BRANCH_WIDTH, D_MODEL), BRANCH_WIDTH ** -0.5),
        "w_out": nrm(ks[19], (L, D_MODEL, D_MODEL), D_MODEL ** -0.5),
        "final_norm_w": 1.0 + nrm(ks[20], (D_MODEL,), 0.02),
    }


def reference(x, norm_w, w_in, fg_bias, ssm_conv_w, ssm_conv_b, dt_bias, a_log, d_skip,
              ssm_norm_w, sc_conv_w, sc_conv_b, cf_conv_w, cf_conv_b, cf_ln_w, cf_ln_b,
              w_gate, b_gate, w_branch, w_out, final_norm_w):
    for l in range(DEPTH):
        x = hybrid_layer(x, norm_w[l], w_in[l], fg_bias[l], ssm_conv_w[l], ssm_conv_b[l],
                         dt_bias[l], a_log[l], d_skip[l], ssm_norm_w[l], sc_conv_w[l],
                         sc_conv_b[l], cf_conv_w[l], cf_conv_b[l], cf_ln_w[l], cf_ln_b[l],
                         w_gate[l], b_gate[l], w_branch[l], w_out[l])
    return rms_norm(x, final_norm_w)
```

```python
import numpy as np
import concourse.bass as bass
import concourse.mybir as mybir
from concourse.bass_utils import run_bass_kernel_spmd

F32 = mybir.dt.float32
BF16 = mybir.dt.bfloat16
AF = mybir.ActivationFunctionType
ALU = mybir.AluOpType

NCORES = 8
D = 2048
T = 16384
SEQ = 8192
L_FULL = 4
EPS = 1e-6
TB = 512
NBLK = T // TB
BPS = SEQ // TB
KT = D // 128
NC1 = 15 * 128 + 3

OFF_Q, OFF_K, OFF_V, OFF_F, OFF_GA = 0, 1024, 2048, 3072, 3080
OFF_Z = 4104
OFF_XBC = 5128
OFF_DT = 6664
OFF_SCB, OFF_SCC, OFF_SCX, OFF_GC = 6680, 7704, 8728, 9752
OFF_GLUA, OFF_GLUG, OFF_GD = 10776, 11800, 12824

CV_NW = 0
CV_SSMW = 2
CV_SSMB = 14
CV_SCW = 17
CV_SCB = 20
CV_CFW = 21
CV_CFB = 52
CV_LNW = 53
CV_LNB = 54
CV_SNW = 55
CV_DSK = 56
CV_BG = 57
CV_SSC = 65
CV_SBI = 66
CV_ALOG = 67
CV_FNW = 68
NV = 70


class Buf:
    def __init__(self, name, ap=None):
        self.name = name
        self.ap = ap
        self.st = {None: ({}, {})}

    def part(self, key):
        return (self, key)


class Op:
    __slots__ = ("eng", "fn", "waits", "signal", "idx", "kind", "sem", "val", "gid")


class Prog:
    ENGS = ("pe", "act", "dve", "pool", "sp")

    def __init__(self, nc):
        self.nc = nc
        self.ops = {e: [] for e in self.ENGS}
        self.gid = 0
        self.seen = {e: {} for e in self.ENGS}
        self.dma_rr = {e: 0 for e in self.ENGS}
        self.dma_cnt = {}
        self.NDMA = {"sp": 20, "pool": 14, "act": 6}
        self.cc_cnt = 0

    @staticmethod
    def _norm(v):
        return v if isinstance(v, tuple) else (v, None)

    def _states(self, buf, key):
        if key is None:
            return list(buf.st.values())
        if key not in buf.st:
            buf.st[key] = ({}, {})
        return [buf.st[None], buf.st[key]]

    def _add(self, eng, reads, writes, fn, kind="c", pe_acc=False):
        op = Op()
        op.eng, op.fn, op.kind = eng, fn, kind
        op.signal = False
        op.idx = len(self.ops[eng])
        op.gid = self.gid
        self.gid += 1
        op.sem = op.val = None
        deps = []
        me = eng if kind == "c" else ("dma", op.gid)
        for v in reads:
            b, k = self._norm(v)
            for (wr, rd) in self._states(b, k):
                deps.extend(wr.values())
        for v in writes:
            b, k = self._norm(v)
            for (wr, rd) in self._states(b, k):
                for pk, p in wr.items():
                    if pk == me:
                        continue
                    deps.append(p)
                for pk, p in rd.items():
                    if pk == me:
                        continue
                    deps.append(p)
        waits = {}
        seen = self.seen[eng]
        for p in deps:
            if p is op:
                continue
            if p.kind == "c":
                key = p.eng
                if seen.get(key, -1) >= p.idx:
                    continue
                if key not in waits or waits[key].idx < p.idx:
                    waits[key] = p
            else:
                key = ("d", p.gid)
                if key in seen:
                    continue
                waits[key] = p
        for key, p in waits.items():
            p.signal = True
            if p.kind == "c":
                seen[key] = p.idx
            else:
                seen[key] = 1
        op.waits = list(waits.values())
        for v in reads:
            b, k = self._norm(v)
            if k is None:
                b.st[None][1][me] = op
            else:
                if k not in b.st:
                    b.st[k] = ({}, {})
                b.st[k][1][me] = op
        for v in writes:
            b, k = self._norm(v)
            if k is None:
                b.st = {None: ({me: op}, {})}
            else:
                b.st[k] = ({me: op}, {})
        self.ops[eng].append(op)
        return op

    def op(self, eng, fn, reads=(), writes=()):
        return self._add(eng, reads, writes, fn, "c")

    def dma(self, eng, out, in_, reads=(), writes=()):
        def fn(e, out=out, in_=in_):
            return e.dma_start(out=out, in_=in_)
        op = self._add(eng, reads, writes, fn, "d")
        op.signal = True
        return op

    def collective(self, ins, outs, reads=(), writes=()):
        def fn(e, ins=ins, outs=outs):
            return e.collective_compute("AllGather", ALU.bypass, replica_groups=[list(range(NCORES))],
                                        ins=[ins], outs=[outs])
        op = self._add("pool", reads, writes, fn, "cc")
        op.signal = True
        return op

    def final_wait(self, eng, ops):
        o = Op()
        o.eng, o.kind, o.signal = eng, "c", False
        o.fn = None
        o.idx = len(self.ops[eng])
        o.gid = self.gid
        self.gid += 1
        for p in ops:
            p.signal = True
        o.waits = list(ops)
        o.sem = o.val = None
        self.ops[eng].append(o)

    def emit(self, block, sems):
        nc = self.nc
        for e in self.ENGS:
            cnt = 0
            for op in self.ops[e]:
                if not op.signal:
                    continue
                if op.kind == "c":
                    cnt += 1
                    op.sem, op.val = sems[e], cnt
                elif op.kind == "d":
                    slot = self.dma_rr[e] % self.NDMA[e]
                    self.dma_rr[e] += 1
                    c = self.dma_cnt.get((e, slot), 0) + 16
                    self.dma_cnt[(e, slot)] = c
                    op.sem, op.val = sems[("d", e, slot)], c
                else:
                    self.cc_cnt += 1
                    op.sem, op.val = sems["cc"], self.cc_cnt

        def run(eng_name):
            def body(e):
                for op in self.ops[eng_name]:
                    wm = {}
                    for p in op.waits:
                        k = id(p.sem)
                        if k not in wm or wm[k][1] < p.val:
                            wm[k] = (p.sem, p.val)
                    for (sm, vl) in wm.values():
                        e.wait_ge(sm, vl)
                    if op.fn is None:
                        continue
                    ins = op.fn(e)
                    if op.signal:
                        if op.kind == "c":
                            ins.then_inc(op.sem, 1)
                        elif op.kind == "d":
                            ins.then_inc(op.sem, 16)
                        else:
                            ins.then_inc(op.sem, 1)
            return body

        block.tensor(run("pe"))
        block.scalar(run("act"))
        block.vector(run("dve"))
        block.gpsimd(run("pool"))
        block.sync(run("sp"))


class Arena:
    def __init__(self, base_ap, nbytes):
        self.base = base_ap
        self.nbytes = nbytes
        self.top = 0
        self.dead = []
        self.live = []

    def alloc(self, name, nbytes, dtype=F32):
        nbytes = (nbytes + 3) // 4 * 4
        off = self.top
        assert off + nbytes <= self.nbytes, f"SBUF arena overflow at {name}: {off + nbytes} > {self.nbytes}"
        self.top += nbytes
        v = self.base[:, off // 4:(off + nbytes) // 4]
        if dtype != F32:
            v = v.bitcast(dtype)
        b = Buf(name, v)
        keep = []
        for (o, s, db) in self.dead:
            if o < off + nbytes and off < o + s:
                wr0, rd0 = b.st[None]
                for (wr, rd) in db.st.values():
                    for k, p in wr.items():
                        rd0[("x", id(db), k)] = p
                    for k, p in rd.items():
                        rd0[("x", id(db), k, "r")] = p
                    for k, p in wr.items():
                        wr0[("x", id(db), k)] = p
            keep.append((o, s, db))
        self.dead = keep
        self.live.append((off, nbytes, b))
        return b

    def mark(self):
        return (self.top, len(self.live))

    def release(self, mark):
        top, n = mark
        for item in self.live[n:]:
            self.dead.append(item)
        self.live = self.live[:n]
        self.top = top


def build_program(NL=L_FULL, taps=()):
    nc = bass.Bass("TRN2", target_bir_lowering=False)
    P = Prog(nc)
    ACT, DVE, PE, POOL, SP = "act", "dve", "pe", "pool", "sp"

    def din(name, shape, dt=F32):
        return nc.dram_tensor(name, list(shape), dt, kind="ExternalInput").ap()

    xT_in = din("xT", [256, T])
    w1_in = din("w1", [NL, D, NC1])
    wg_in = din("wg", [NL, 4, D, 256])
    wb_in = din("wb", [NL, 4, 1024, 256])
    wo_in = din("wo", [NL, D, 256])
    cvec_in = din("cvec", [128, NL * NV])
    cst_in = din("cst", [128, 1024])
    out_ext = nc.dram_tensor("outT", [256, T], F32, kind="ExternalOutput").ap()
    tap_out = {}
    for (nm, shp, dt) in taps:
        tap_out[nm] = nc.dram_tensor("tap_" + nm, list(shp), dt, kind="ExternalOutput").ap()

    def dscr(name, shape, dt):
        t = nc.dram_tensor(name, list(shape), dt)
        return t

    xres_t = dscr("xres", [256, T], F32)
    agt = {}
    for nm, rows, dt_ in (("rs_in", 1, F32), ("rs_out", 8, F32), ("h_in", 256, BF16), ("h_out", D, BF16),
                          ("ms_in", 3, F32), ("ms_out", 24, F32), ("y_in", 512, BF16), ("y_out", 4096, BF16),
                          ("m_in", 256, BF16), ("m_out", D, BF16)):
        for b_ in range(2):
            agt[nm + str(b_)] = dscr(nm + str(b_), [rows, SEQ], dt_)
    sp_q_t = dscr("sp_q", [128, T], BF16)
    sp_k_t = dscr("sp_k", [128, T], BF16)
    sp_v_t = dscr("sp_v", [T, 128], BF16)
    sp_ga_t = dscr("sp_ga", [128, T], BF16)
    sp_zs_t = dscr("sp_zs", [128, T], BF16)
    sp_xs_t = dscr("sp_xs", [128, T], BF16)
    sp_bs_t = dscr("sp_bs", [128, T], BF16)
    sp_cs_t = dscr("sp_cs", [128, T], BF16)
    sp_cf_t = dscr("sp_cf", [128, T], F32)
    sp_gd_t = dscr("sp_gd", [128, T], BF16)
    sp_yp_t = dscr("sp_yp", [128, T], F32)
    sp_rows_t = dscr("sp_rows", [3, T], F32)
    sp_c3_t = dscr("sp_c3", [3, T], BF16)

    DB = {}
    for nm, t in list(agt.items()) + [("xres", xres_t), ("sp_q", sp_q_t),
                  ("sp_k", sp_k_t), ("sp_v", sp_v_t), ("sp_ga", sp_ga_t), ("sp_zs", sp_zs_t),
                  ("sp_xs", sp_xs_t), ("sp_bs", sp_bs_t), ("sp_cs", sp_cs_t), ("sp_cf", sp_cf_t),
                  ("sp_gd", sp_gd_t), ("sp_yp", sp_yp_t), ("sp_rows", sp_rows_t), ("sp_c3", sp_c3_t)]:
        DB[nm] = Buf(nm, t.ap())
    DB["out"] = Buf("out", out_ext)
    for nm in tap_out:
        DB["tap_" + nm] = Buf("tap_" + nm, tap_out[nm])

    ARENA_BYTES = 207 * 1024
    import contextlib
    with contextlib.ExitStack() as es:
        arena_t = es.enter_context(nc.sbuf_tensor("arena", [128, ARENA_BYTES // 4], F32))
        psum_t = es.enter_context(nc.psum_tensor("ps", [128, 8, 512], F32))
        sems = {}
        for e in ("pe", "act", "dve", "pool", "sp"):
            sems[e] = es.enter_context(nc.semaphore("c_" + e))
        for e, n in P.NDMA.items():
            for i in range(n):
                sems[("d", e, i)] = es.enter_context(nc.semaphore(f"d_{e}_{i}"))
        sems["cc"] = es.enter_context(nc.semaphore("cc"))

        A = Arena(arena_t[:, :], ARENA_BYTES)
        banks = [Buf(f"bank{i}", psum_t[:, i, :]) for i in range(8)]
        bank_rr = [0]

        def bank():
            b = banks[bank_rr[0] % 8]
            bank_rr[0] += 1
            return b

        cvec = A.alloc("cvec", NL * NV * 4)
        cst = A.alloc("cst", 1024 * 4)
        cstb = A.alloc("cstb", 1024 * 2, BF16)
        TM = A.alloc("TM", 128 * 6 * 4)
        TMv = TM.ap.rearrange("p (c s) -> p c s", s=6)
        negA = A.alloc("negA", 4 * 4)
        P.dma(SP, cvec.ap, cvec_in, writes=[cvec])
        P.dma(SP, cst.ap, cst_in, writes=[cst])
        P.op(DVE, lambda e: e.tensor_copy(out=cstb.ap, in_=cst.ap), reads=[cst], writes=[cstb])
        ident_f = cst.ap[:, 0:128]
        ident_b = cstb.ap[:, 0:128]
        tri01 = cst.ap[:, 128:256]
        trineg_b = cstb.ap[:, 256:384]
        ones_f = cst.ap[:, 384:512]
        ones_b = cstb.ap[:, 384:512]
        cmask = cst.ap[0:3, 512:1024]

        def cv(l, col, n=1, rows=slice(0, 128)):
            return cvec.ap[rows, l * NV + col:l * NV + col + n]

        def tap(name, src_buf, src_ap, dst_ap):
            if ("tap_" + name) in DB:
                P.dma(SP, dst_ap, src_ap, reads=[src_buf], writes=[DB["tap_" + name].part(str(dst_ap.offset))])

        def stage_rms_stats_from(xsrc_buf, xsrc_ap):
            mk = A.mark()
            xb = [A.alloc(f"s0x{i}", 2 * TB * 4) for i in range(2)]
            sq = [A.alloc(f"s0q{i}", 2 * TB * 4) for i in range(2)]
            row = [A.alloc(f"s0r{i}", TB * 4) for i in range(2)]
            for blk in range(NBLK):
                t0 = blk * TB
                lt0 = (blk % BPS) * TB
                sb_ = str(blk // BPS)
                x_, q_, r_ = xb[blk % 2], sq[blk % 2], row[blk % 2]
                xv = x_.ap.rearrange("p (c t) -> p c t", c=2)
                qv = q_.ap.rearrange("p (c t) -> p c t", c=2)
                P.dma(SP, xv, xsrc_ap.rearrange("(c p) t -> p c t", p=128)[:, :, t0:t0 + TB],
                      reads=[(xsrc_buf, blk)], writes=[x_])
                P.op(ACT, lambda e, a=q_.ap, b=x_.ap: e.activation(out=a, in_=b, func=AF.Square),
                     reads=[x_], writes=[q_])
                bk = bank()
                for c in range(2):
                    P.op(PE, lambda e, o=bk.ap[0:1, :], r=qv[:, c, :], c=c: e.matmul(o, ones_f[:, 0:1], r, start=(c == 0), stop=(c == 1)),
                         reads=[q_, cst], writes=[bk])
                P.op(ACT, lambda e, o=r_.ap[0:1, :], i=bk.ap[0:1, :]: e.copy(out=o, in_=i), reads=[bk], writes=[r_])
                P.dma(POOL, DB["rs_in" + sb_].ap[0:1, lt0:lt0 + TB], r_.ap[0:1, :], reads=[r_], writes=[(DB["rs_in" + sb_], blk)])
                ag_if_last("rs_in", "rs_out", blk)
            A.release(mk)

        def ag(src, dst, b_):
            src, dst = src + str(b_), dst + str(b_)
            P.collective(DB[src].ap.tensor.ap().opt(), DB[dst].ap.tensor.ap().opt(), reads=[DB[src]], writes=[DB[dst]])

        def ag_if_last(src, dst, blk):
            if blk % BPS == BPS - 1:
                ag(src, dst, blk // BPS)

        def stage_norm(l, xsrc_buf, xsrc_ap, final=False):
            mk = A.mark()
            xb = [A.alloc(f"s1x{i}", 2 * TB * 4) for i in range(4)]
            sg = [A.alloc(f"s1g{i}", TB * 4) for i in range(4)]
            rb = [A.alloc(f"s1r{i}", TB * 4) for i in range(4)]
            hb = [A.alloc(f"s1h{i}", 2 * TB * 4) for i in range(4)]
            wcol = CV_FNW if final else CV_NW
            ll = 0 if final else l
            for blk in range(NBLK):
                t0 = blk * TB
                lt0 = (blk % BPS) * TB
                sb_ = str(blk // BPS)
                x_, g_, r_, h_ = xb[blk % 4], sg[blk % 4], rb[blk % 4], hb[blk % 4]
                xv = x_.ap.rearrange("p (c t) -> p c t", c=2)
                P.dma(SP, xv, xsrc_ap.rearrange("(c p) t -> p c t", p=128)[:, :, t0:t0 + TB],
                      reads=[(xsrc_buf, blk)], writes=[x_])
                P.dma(SP, g_.ap[0:8, :], DB["rs_out" + sb_].ap[:, lt0:lt0 + TB], reads=[DB["rs_out" + sb_]], writes=[g_])
                bk = bank()
                P.op(PE, lambda e, o=bk.ap, r=g_.ap[0:8, :]: e.matmul(o, ones_f[0:8, :], r, start=True, stop=True),
                     reads=[g_, cst], writes=[bk])
                P.op(ACT, lambda e, o=r_.ap, i=bk.ap: e.activation(out=o, in_=i, func=AF.Sqrt, bias=EPS, scale=1.0 / D),
                     reads=[bk], writes=[r_])
                P.op(DVE, lambda e, o=r_.ap: e.reciprocal(out=o, in_=o), reads=[r_], writes=[r_])
                if final:
                    hv = h_.ap.rearrange("p (c t) -> p c t", c=2)
                else:
                    hv = h_.ap.bitcast(BF16)[:, 0:2 * TB].rearrange("p (c t) -> p c t", c=2)
                for c in range(2):
                    P.op(DVE, lambda e, o=hv[:, c, :], i=xv[:, c, :], w=cv(ll, wcol + c), r=r_.ap:
                         e.scalar_tensor_tensor(out=o, in0=i, scalar=w, in1=r, op0=ALU.mult, op1=ALU.mult),
                         reads=[x_, r_, cvec], writes=[h_])
                if final:
                    P.dma(POOL, DB["out"].ap.rearrange("(c p) t -> p c t", p=128)[:, :, t0:t0 + TB], hv,
                          reads=[h_], writes=[(DB["out"], blk)])
                else:
                    P.dma(POOL, DB["h_in" + sb_].ap.rearrange("(c p) t -> p c t", p=128)[:, :, lt0:lt0 + TB], hv,
                          reads=[h_], writes=[(DB["h_in" + sb_], blk)])
                    ag_if_last("h_in", "h_out", blk)
            A.release(mk)

        def load_w1(l, W1):
            src = w1_in[l].rearrange("(k p) n -> p k n", p=128)
            dst = W1.ap.rearrange("p (k n) -> p k n", k=KT)
            for k in range(KT):
                P.dma(POOL, dst[:, k, :], src[:, k, :], writes=[(W1, k)])

        def stage_inproj(l, W1):
            mk = A.mark()
            W1v = W1.ap.rearrange("p (k n) -> p k n", k=KT)
            hT = [A.alloc(f"s2h{i}", KT * TB * 2, BF16) for i in range(2)]
            stg_ssm = [[A.alloc(f"s2ssm{i}{j}", (3 + TB) * 4) for j in range(3)] for i in range(2)]
            stg_sc = [A.alloc(f"s2sc{i}", (2 + TB) * 4) for i in range(2)]
            stg_cf = [A.alloc(f"s2cf{i}", (30 + TB) * 4) for i in range(2)]
            acc = [A.alloc(f"s2acc{i}", TB * 4) for i in range(10)]
            deferred = []
            f32t = [A.alloc(f"s2f{i}", TB * 4) for i in range(4)]
            bft = [A.alloc(f"s2b{i}", TB * 2, BF16) for i in range(8)]
            vsb = [A.alloc(f"s2v{i}", TB * 2, BF16) for i in range(2)]
            smu = [A.alloc(f"s2su{i}", TB * 4) for i in range(2)]
            smw = [A.alloc(f"s2sw{i}", TB * 4) for i in range(2)]
            smm = [A.alloc(f"s2sm{i}", TB * 4) for i in range(2)]
            smr = [A.alloc(f"s2sr{i}", TB * 4) for i in range(2)]
            c3t = [A.alloc(f"s2c3{i}", TB * 4) for i in range(3)]
            c3b = [A.alloc(f"s2c3b{i}", TB * 2, BF16) for i in range(3)]
            strow = [A.alloc(f"s2st{i}", TB * 4) for i in range(2)]
            rr = {"acc": 0, "f": 0, "b": 0}

            def nxt(lst, key):
                b = lst[rr[key] % len(lst)]
                rr[key] += 1
                return b

            P.op(ACT, lambda e: e.activation(out=negA.ap[0:3, 0:1], in_=cv(l, CV_ALOG, 1, slice(0, 3)), func=AF.Exp),
                 reads=[cvec], writes=[negA])
            P.op(DVE, lambda e: e.tensor_scalar(out=negA.ap[0:3, 0:1], in0=negA.ap[0:3, 0:1], scalar1=-1.0, scalar2=None, op0=ALU.mult),
                 reads=[negA], writes=[negA])
            P.op(DVE, lambda e: e.tensor_tensor(out=negA.ap[0:3, 1:2], in0=cv(l, CV_SBI, 1, slice(0, 3)), in1=cv(l, CV_SSC, 1, slice(0, 3)), op=ALU.mult),
                 reads=[cvec], writes=[(negA, "b")])

            def mm_tile(ct, h_, ncols=128):
                bk = bank()
                for k in range(KT):
                    P.op(PE, lambda e, o=bk.ap[0:ncols, :], w=W1v[:, k, ct * 128:ct * 128 + ncols], r=h_.ap.rearrange("p (k t) -> p k t", k=KT)[:, k, :], k=k:
                         e.matmul(o, w, r, start=(k == 0), stop=(k == KT - 1)),
                         reads=[h_, (W1, k)], writes=[bk])
                return bk

            def spill(dst, blk, src_buf, src_ap):
                t0 = blk * TB
                lt0 = (blk % BPS) * TB
                sb_ = str(blk // BPS)
                P.dma(SP, DB[dst].ap[:, t0:t0 + TB], src_ap, reads=[src_buf], writes=[(DB[dst], blk)])

            for blk in range(NBLK):
                t0 = blk * TB
                lt0 = (blk % BPS) * TB
                sb_ = str(blk // BPS)
                first = (blk % BPS == 0)
                h_ = hT[blk % 2]
                P.dma(SP, h_.ap.rearrange("p (k t) -> p k t", k=KT),
                      DB["h_out" + sb_].ap.rearrange("(k p) t -> p k t", p=128)[:, :, lt0:lt0 + TB],
                      reads=[DB["h_out" + sb_]], writes=[h_])
                for ct, dst, func, scale in ((0, "sp_q", AF.Copy, 128.0 ** -0.5), (1, "sp_k", AF.Copy, 1.0),
                                             (2, "sp_ga", AF.Silu, 1.0), (3, "sp_zs", AF.Silu, 1.0)):
                    bk = mm_tile(ct, h_)
                    ob = nxt(bft, "b")
                    P.op(ACT, lambda e, o=ob.ap, i=bk.ap, f=func, s=scale: e.activation(out=o, in_=i, func=f, scale=s),
                         reads=[bk], writes=[ob])
                    spill(dst, blk, ob, ob.ap)
                for j, (ct, dst) in enumerate(((4, "sp_xs"), (5, "sp_bs"), (6, "sp_cs"))):
                    bk = mm_tile(ct, h_)
                    sg_, sp_ = stg_ssm[blk % 2][j], stg_ssm[(blk + 1) % 2][j]
                    P.op(ACT, lambda e, o=sg_.ap[:, 3:3 + TB], i=bk.ap: e.copy(out=o, in_=i), reads=[bk], writes=[sg_])
                    if first:
                        P.op(DVE, lambda e, o=sg_.ap[:, 0:3]: e.memset(o, 0.0), writes=[sg_])
                    else:
                        P.op(DVE, lambda e, o=sg_.ap[:, 0:3], i=sp_.ap[:, TB:TB + 3]: e.tensor_copy(out=o, in_=i), reads=[sp_], writes=[sg_])
                    ac = nxt(acc, "acc")
                    for tp in range(4):
                        w = cv(l, CV_SSMW + 4 * j + tp)
                        if tp == 0:
                            P.op(DVE, lambda e, o=ac.ap, i=sg_.ap[:, 0:TB], w=w: e.tensor_scalar(out=o, in0=i, scalar1=w, scalar2=None, op0=ALU.mult),
                                 reads=[sg_, cvec], writes=[ac])
                        else:
                            P.op(DVE, lambda e, o=ac.ap, i=sg_.ap[:, tp:tp + TB], w=w:
                                 e.scalar_tensor_tensor(out=o, in0=i, scalar=w, in1=o, op0=ALU.mult, op1=ALU.add),
                                 reads=[sg_, cvec, ac], writes=[ac])
                    def d_silu(ac=ac, j=j, dst=dst, blk=blk):
                        ob = nxt(bft, "b")
                        P.op(ACT, lambda e, o=ob.ap, i=ac.ap, b=cv(l, CV_SSMB + j): e.activation(out=o, in_=i, func=AF.Silu, bias=b),
                             reads=[ac, cvec], writes=[ob])
                        spill(dst, blk, ob, ob.ap)
                    deferred.append(d_silu)
                for fn_ in deferred:
                    fn_()
                deferred.clear()
                bk_b = mm_tile(7, h_)
                bk_c = mm_tile(8, h_)
                bk_x = mm_tile(9, h_)
                fx = nxt(f32t, "f")
                P.op(ACT, lambda e, o=fx.ap, i=bk_x.ap: e.copy(out=o, in_=i), reads=[bk_x], writes=[fx])
                sg_, sp_ = stg_sc[blk % 2], stg_sc[(blk + 1) % 2]
                P.op(DVE, lambda e, o=sg_.ap[:, 2:2 + TB], a=bk_c.ap, b=fx.ap: e.tensor_tensor(out=o, in0=a, in1=b, op=ALU.mult),
                     reads=[bk_c, fx], writes=[sg_])
                if first:
                    P.op(DVE, lambda e, o=sg_.ap[:, 0:2]: e.memset(o, 0.0), writes=[sg_])
                else:
                    P.op(DVE, lambda e, o=sg_.ap[:, 0:2], i=sp_.ap[:, TB:TB + 2]: e.tensor_copy(out=o, in_=i), reads=[sp_], writes=[sg_])
                ac = nxt(acc, "acc")
                for tp in range(3):
                    w = cv(l, CV_SCW + tp)
                    if tp == 0:
                        P.op(DVE, lambda e, o=ac.ap, i=sg_.ap[:, 0:TB], w=w: e.tensor_scalar(out=o, in0=i, scalar1=w, scalar2=None, op0=ALU.mult),
                             reads=[sg_, cvec], writes=[ac])
                    else:
                        P.op(DVE, lambda e, o=ac.ap, i=sg_.ap[:, tp:tp + TB], w=w:
                             e.scalar_tensor_tensor(out=o, in0=i, scalar=w, in1=o, op0=ALU.mult, op1=ALU.add),
                             reads=[sg_, cvec, ac], writes=[ac])
                P.op(DVE, lambda e, o=ac.ap, b=cv(l, CV_SCB), i1=bk_b.ap:
                     e.scalar_tensor_tensor(out=o, in0=o, scalar=b, in1=i1, op0=ALU.add, op1=ALU.mult),
                     reads=[ac, cvec, bk_b], writes=[ac])
                bk_g = mm_tile(10, h_)
                fg = nxt(f32t, "f")
                P.op(ACT, lambda e, o=fg.ap, i=bk_g.ap: e.activation(out=o, in_=i, func=AF.Silu), reads=[bk_g], writes=[fg])
                ob = nxt(bft, "b")
                P.op(DVE, lambda e, o=ob.ap, a=ac.ap, b=fg.ap: e.tensor_tensor(out=o, in0=a, in1=b, op=ALU.mult),
                     reads=[ac, fg], writes=[ob])
                P.dma(POOL, DB["y_in" + sb_].ap[256:384, lt0:lt0 + TB], ob.ap, reads=[ob], writes=[(DB["y_in" + sb_], (2, blk))])
                bk_a = mm_tile(11, h_)
                bk_gg = mm_tile(12, h_)
                fs = nxt(f32t, "f")
                P.op(ACT, lambda e, o=fs.ap, i=bk_gg.ap: e.activation(out=o, in_=i, func=AF.Sigmoid), reads=[bk_gg], writes=[fs])
                sg_, sp_ = stg_cf[blk % 2], stg_cf[(blk + 1) % 2]
                P.op(DVE, lambda e, o=sg_.ap[:, 30:30 + TB], a=bk_a.ap, b=fs.ap: e.tensor_tensor(out=o, in0=a, in1=b, op=ALU.mult),
                     reads=[bk_a, fs], writes=[sg_])
                if first:
                    P.op(DVE, lambda e, o=sg_.ap[:, 0:30]: e.memset(o, 0.0), writes=[sg_])
                else:
                    P.op(DVE, lambda e, o=sg_.ap[:, 0:30], i=sp_.ap[:, TB:TB + 30]: e.tensor_copy(out=o, in_=i), reads=[sp_], writes=[sg_])
                ac = nxt(acc, "acc")
                for tp in range(31):
                    w = cv(l, CV_CFW + tp)
                    if tp == 0:
                        P.op(DVE, lambda e, o=ac.ap, i=sg_.ap[:, 0:TB], w=w, b=cv(l, CV_CFB):
                             e.tensor_scalar(out=o, in0=i, scalar1=w, scalar2=b, op0=ALU.mult, op1=ALU.add),
                             reads=[sg_, cvec], writes=[ac])
                    else:
                        P.op(DVE, lambda e, o=ac.ap, i=sg_.ap[:, tp:tp + TB], w=w:
                             e.scalar_tensor_tensor(out=o, in0=i, scalar=w, in1=o, op0=ALU.mult, op1=ALU.add),
                             reads=[sg_, cvec, ac], writes=[ac])
                spill("sp_cf", blk, ac, ac.ap)
                def d_stats(ac=ac, blk=blk, lt0=lt0, sb_=sb_):
                    fq = nxt(f32t, "f")
                    P.op(ACT, lambda e, o=fq.ap, i=ac.ap: e.activation(out=o, in_=i, func=AF.Square), reads=[ac], writes=[fq])
                    bk1 = bank()
                    P.op(PE, lambda e, o=bk1.ap[0:1, :], r=ac.ap: e.matmul(o, ones_f[:, 0:1], r, start=True, stop=True), reads=[ac, cst], writes=[bk1])
                    bk2 = bank()
                    P.op(PE, lambda e, o=bk2.ap[0:1, :], r=fq.ap: e.matmul(o, ones_f[:, 0:1], r, start=True, stop=True), reads=[fq, cst], writes=[bk2])
                    for bkx, row in ((bk1, 1), (bk2, 2)):
                        sr = strow[row - 1]
                        P.op(ACT, lambda e, o=sr.ap[0:1, :], i=bkx.ap[0:1, :]: e.copy(out=o, in_=i), reads=[bkx], writes=[sr])
                        P.dma(POOL, DB["ms_in" + sb_].ap[row:row + 1, lt0:lt0 + TB], sr.ap[0:1, :], reads=[sr], writes=[(DB["ms_in" + sb_], (row, blk))])
                deferred.append(d_stats)
                bk_d = mm_tile(13, h_)
                ob = nxt(bft, "b")
                P.op(ACT, lambda e, o=ob.ap, i=bk_d.ap: e.activation(out=o, in_=i, func=AF.Silu), reads=[bk_d], writes=[ob])
                spill("sp_gd", blk, ob, ob.ap)
                v_ = vsb[blk % 2]
                hv = h_.ap.rearrange("p (k t) -> p k t", k=KT)
                for s in range(4):
                    bk = bank()
                    for k in range(KT):
                        P.op(PE, lambda e, o=bk.ap[:, 0:128], w=hv[:, k, s * 128:(s + 1) * 128], r=W1v[:, k, 14 * 128:15 * 128], k=k:
                             e.matmul(o, w, r, start=(k == 0), stop=(k == KT - 1)), reads=[h_, (W1, k)], writes=[bk])
                    P.op(ACT, lambda e, o=v_.ap[:, s * 128:(s + 1) * 128], i=bk.ap[:, 0:128]: e.copy(out=o, in_=i), reads=[bk], writes=[v_])
                P.dma(POOL, DB["sp_v"].ap[t0:t0 + TB, :].rearrange("(s p) d -> p s d", p=128),
                      v_.ap.rearrange("p (s d) -> p s d", s=4), reads=[v_], writes=[(DB["sp_v"], blk)])
                bk = mm_tile(15, h_, ncols=3)
                u_, w_, m_, r_, rp_ = smu[blk % 2], smw[blk % 2], smm[blk % 2], smr[blk % 2], smr[(blk + 1) % 2]
                P.op(ACT, lambda e, o=u_.ap[0:3, :], i=bk.ap[0:3, :]: e.activation(out=o, in_=i, func=AF.Exp, bias=negA.ap[0:3, 1:2], scale=cv(l, CV_SSC, 1, slice(0, 3))),
                     reads=[bk, cvec, (negA, "b")], writes=[u_])
                P.op(ACT, lambda e, o=w_.ap[0:3, :], i=u_.ap[0:3, :]: e.activation(out=o, in_=i, func=AF.Ln, bias=1.0),
                     reads=[u_], writes=[w_])
                P.op(DVE, lambda e, o=m_.ap[0:3, :], i=w_.ap[0:3, :]: e.tensor_scalar(out=o, in0=i, scalar1=negA.ap[0:3, 0:1], scalar2=None, op0=ALU.mult),
                     reads=[w_, negA], writes=[m_])
                init = 0.0 if first else rp_.ap[0:3, TB - 1:TB]
                P.op(DVE, lambda e, o=r_.ap[0:3, :], d1=m_.ap[0:3, :], init=init: e.tensor_tensor_scan(out=o, data0=cmask, data1=d1, initial=init, op0=ALU.mult, op1=ALU.add),
                     reads=[m_, cst] + ([] if first else [rp_]), writes=[r_])
                P.dma(POOL, DB["sp_rows"].ap[:, t0:t0 + TB], r_.ap[0:3, :], reads=[r_], writes=[(DB["sp_rows"], blk)])
                P.op(DVE, lambda e, o=c3b[0].ap[0:3, :], i=r_.ap[0:3, :]: e.tensor_copy(out=o, in_=i), reads=[r_], writes=[c3b[0]])
                P.op(DVE, lambda e, o=c3t[0].ap[0:3, :], a=r_.ap[0:3, :], b=c3b[0].ap[0:3, :]: e.tensor_tensor(out=o, in0=a, in1=b, op=ALU.subtract),
                     reads=[r_, c3b[0]], writes=[c3t[0]])
                P.op(DVE, lambda e, o=c3b[1].ap[0:3, :], i=c3t[0].ap[0:3, :]: e.tensor_copy(out=o, in_=i), reads=[c3t[0]], writes=[c3b[1]])
                P.op(DVE, lambda e, o=c3t[1].ap[0:3, :], a=c3t[0].ap[0:3, :], b=c3b[1].ap[0:3, :]: e.tensor_tensor(out=o, in0=a, in1=b, op=ALU.subtract),
                     reads=[c3t[0], c3b[1]], writes=[c3t[1]])
                P.op(DVE, lambda e, o=c3b[2].ap[0:3, :], i=c3t[1].ap[0:3, :]: e.tensor_copy(out=o, in_=i), reads=[c3t[1]], writes=[c3b[2]])
                for i in range(3):
                    P.dma(POOL, DB["sp_c3"].ap[i:i + 1, t0:t0 + TB], c3b[i].ap[0:1, :], reads=[c3b[i]], writes=[(DB["sp_c3"], (i, blk))])
                def d_tm(blk=blk, r_=r_, w_=w_):
                    for s in range(4):
                        gc = blk * 4 + s
                        bk = bank()
                        P.op(PE, lambda e, o=bk.ap[:, 0:3], w=r_.ap[0:3, s * 128:(s + 1) * 128]: e.matmul(o, w, ident_f[0:3, 0:3], start=True, stop=True),
                             reads=[r_, cst], writes=[bk])
                        P.op(PE, lambda e, o=bk.ap[:, 4:7], w=w_.ap[0:3, s * 128:(s + 1) * 128]: e.matmul(o, w, ident_f[0:3, 0:3], start=True, stop=True),
                             reads=[w_, cst], writes=[bk])
                        P.op(ACT, lambda e, o=TMv[:, gc, 0:3], i=bk.ap[:, 0:3]: e.activation(out=o, in_=i, func=AF.Copy, scale=-1.0),
                             reads=[bk], writes=[(TM, gc)])
                        P.op(ACT, lambda e, o=TMv[:, gc, 3:6], i=bk.ap[:, 4:7]: e.copy(out=o, in_=i),
                             reads=[bk], writes=[(TM, gc)])
                deferred.append(d_tm)
            for fn_ in deferred:
                fn_()
            deferred.clear()
            A.release(mk)

        def stage_attention():
            mk = A.mark()
            qT = A.alloc("a_q", SEQ * 2, BF16)
            kT = A.alloc("a_k", SEQ * 2, BF16)
            vt = A.alloc("a_v", SEQ * 2, BF16)
            c3 = A.alloc("a_c3", SEQ * 2, BF16)
            pt = [A.alloc(f"a_p{i}", TB * 2, BF16) for i in range(3)]
            ga = [A.alloc(f"a_ga{i}", TB * 2, BF16) for i in range(2)]
            rc = [A.alloc(f"a_rc{i}", TB * 4) for i in range(2)]
            ot = [A.alloc(f"a_o{i}", TB * 4) for i in range(2)]
            yb = [A.alloc(f"a_y{i}", TB * 2, BF16) for i in range(2)]
            vv = vt.ap.rearrange("p (s d) -> p s d", d=128)
            pi = 0
            acc_i = 0
            for b in range(2):
                s0 = b * SEQ
                for i in range(4):
                    sl = slice(i * 2048, (i + 1) * 2048)
                    P.dma(SP, qT.ap[:, sl], DB["sp_q"].ap[:, s0 + i * 2048:s0 + (i + 1) * 2048], reads=[DB["sp_q"]], writes=[(qT, i)])
                    P.dma(SP, kT.ap[:, sl], DB["sp_k"].ap[:, s0 + i * 2048:s0 + (i + 1) * 2048], reads=[DB["sp_k"]], writes=[(kT, i)])
                    P.dma(SP, vv[:, i * 16:(i + 1) * 16, :],
                          DB["sp_v"].ap[s0 + i * 2048:s0 + (i + 1) * 2048, :].rearrange("(s p) d -> p s d", p=128),
                          reads=[DB["sp_v"]], writes=[(vt, i)])
                P.dma(SP, c3.ap[0:3, :], DB["sp_c3"].ap[:, s0:s0 + SEQ], reads=[DB["sp_c3"]], writes=[c3])
                for qb in range(BPS):
                    blk = b * BPS + qb
                    q0 = qb * TB
                    g_ = ga[qb % 2]
                    P.dma(SP, g_.ap, DB["sp_ga"].ap[:, s0 + q0:s0 + q0 + TB], reads=[(DB["sp_ga"], blk)], writes=[g_])
                    bo, br = banks[(acc_i % 2) * 2], banks[(acc_i % 2) * 2 + 1]
                    acc_i += 1
                    nk = 4 * qb + 4
                    def qk(j):
                        nonlocal pi
                        m = j - 4 * qb
                        c0 = 128 * m if m > 0 else 0
                        bs = banks[4 + pi % 4]
                        p_ = pt[pi % 3]
                        pi += 1
                        P.op(PE, lambda e, o=bs.ap[:, c0:TB], w=kT.ap[:, j * 128:(j + 1) * 128], r=qT.ap[:, q0 + c0:q0 + TB]:
                             e.matmul(o, w, r, start=True, stop=False), reads=[qT, kT], writes=[bs])
                        last = (m < 0)
                        P.op(PE, lambda e, o=bs.ap[:, c0:TB], r=c3.ap[0:3, q0 + c0:q0 + TB], last=last:
                             e.matmul(o, ones_b[0:3, :], r, start=False, stop=last), reads=[c3, cstb], writes=[bs])
                        if m >= 0:
                            P.op(PE, lambda e, o=bs.ap[:, c0:c0 + 128]: e.matmul(o, ident_b, trineg_b, start=False, stop=True),
                                 reads=[cstb], writes=[bs])
                        gck = b * 64 + j
                        P.op(ACT, lambda e, o=p_.ap[:, c0:TB], i=bs.ap[:, c0:TB], bia=TMv[:, gck, 0:1]:
                             e.activation(out=o, in_=i, func=AF.Exp, bias=bia), reads=[bs, (TM, gck)], writes=[p_])
                        return (j, c0, p_)

                    def pv(st):
                        j, c0, p_ = st
                        P.op(PE, lambda e, o=bo.ap[:, c0:TB], w=vv[:, j, :], r=p_.ap[:, c0:TB], j=j, nk=nk:
                             e.matmul(o, w, r, start=(j == 0), stop=(j == nk - 1)), reads=[vt, p_], writes=[bo])
                        P.op(PE, lambda e, o=br.ap[:, c0:TB], r=p_.ap[:, c0:TB], j=j, nk=nk:
                             e.matmul(o, ones_b, r, start=(j == 0), stop=(j == nk - 1)), reads=[cstb, p_], writes=[br])

                    prev = None
                    for j in range(nk + 1):
                        cur = qk(j) if j < nk else None
                        if prev is not None:
                            pv(prev)
                        prev = cur
                    r_, o_, y_ = rc[qb % 2], ot[qb % 2], yb[qb % 2]
                    P.op(DVE, lambda e, o=r_.ap, i=br.ap: e.reciprocal(out=o, in_=i), reads=[br], writes=[r_])
                    P.op(DVE, lambda e, o=o_.ap, a=bo.ap, b_=r_.ap: e.tensor_tensor(out=o, in0=a, in1=b_, op=ALU.mult), reads=[bo, r_], writes=[o_])
                    P.op(DVE, lambda e, o=y_.ap, a=o_.ap, b_=g_.ap: e.tensor_tensor(out=o, in0=a, in1=b_, op=ALU.mult), reads=[o_, g_], writes=[y_])
                    t0 = blk * TB
                    lt0 = (blk % BPS) * TB
                    sb_ = str(blk // BPS)
                    P.dma(POOL, DB["y_in" + sb_].ap[0:128, lt0:lt0 + TB], y_.ap, reads=[y_], writes=[(DB["y_in" + sb_], (0, blk))])
            A.release(mk)

        def stage_ssd(l):
            mk = A.mark()
            selh = A.alloc("d_sel", 2 * 128 * 4)
            selv = selh.ap.rearrange("p (h m) -> p h m", h=2)
            for h in range(2):
                P.op(DVE, lambda e, o=selv[0:3, h, :], h=h: e.tensor_scalar(out=o, in0=ones_f[0:3, :], scalar1=ident_f[0:3, 1 + h:2 + h], scalar2=None, op0=ALU.mult),
                     reads=[cst], writes=[selh])

            def stream(b, B4):
                pf = f"d{b}_"
                xs = [A.alloc(pf + f"xs{i}", TB * 2, BF16) for i in range(2)]
                bsb = [A.alloc(pf + f"bs{i}", TB * 2, BF16) for i in range(2)]
                csb = [A.alloc(pf + f"cs{i}", TB * 2, BF16) for i in range(2)]
                zsb = [A.alloc(pf + f"zs{i}", TB * 2, BF16) for i in range(2)]
                rw = [A.alloc(pf + f"rw{i}", TB * 4) for i in range(2)]
                yp = [A.alloc(pf + f"yp{i}", TB * 4) for i in range(2)]
                sq = A.alloc(pf + "sq", TB * 4)
                strow = A.alloc(pf + "st", TB * 4)
                H = A.alloc(pf + "H", 128 * 4)
                Hb = [A.alloc(pf + f"Hb{i}", 128 * 2, BF16) for i in range(2)]
                xtok = [A.alloc(pf + f"xt{i}", 128 * 2, BF16) for i in range(2)]
                xd = [A.alloc(pf + f"xd{i}", 128 * 2, BF16) for i in range(2)]
                btok = [A.alloc(pf + f"bt{i}", 128 * 2, BF16) for i in range(2)]
                cbm = [A.alloc(pf + f"cbm{i}", 128 * 4) for i in range(2)]
                bcs = [A.alloc(pf + f"bc{i}", 128 * 4) for i in range(4)]
                seg = [A.alloc(pf + f"sg{i}", 128 * 4) for i in range(4)]
                Mh = [A.alloc(pf + f"M{i}", 128 * 2, BF16) for i in range(4)]
                ecs = [A.alloc(pf + f"ec{i}", 128 * 4) for i in range(4)]
                csc = [A.alloc(pf + f"cc{i}", 128 * 2, BF16) for i in range(4)]
                dcd = [A.alloc(pf + f"dc{i}", 4 * 4) for i in range(4)]
                t1 = [A.alloc(pf + f"t1{i}", 128 * 4) for i in range(2)]
                yield
                ci = 0
                P.op(DVE, lambda e: e.memset(H.ap, 0.0), writes=[H])
                P.op(DVE, lambda e: e.memset(Hb[0].ap, 0.0), writes=[Hb[0]])
                hb_i = 0
                for qb in range(BPS):
                    blk = b * BPS + qb
                    t0 = blk * TB
                    lt0 = (blk % BPS) * TB
                    sb_ = str(blk // BPS)
                    x_, b_, c_, z_, r_, y_ = xs[blk % 2], bsb[blk % 2], csb[blk % 2], zsb[blk % 2], rw[blk % 2], yp[blk % 2]
                    for dst, src in ((x_, "sp_xs"), (b_, "sp_bs"), (c_, "sp_cs"), (z_, "sp_zs")):
                        P.dma(SP, dst.ap, DB[src].ap[:, t0:t0 + TB], reads=[(DB[src], blk)], writes=[dst])
                    P.dma(SP, r_.ap[0:3, :], DB["sp_rows"].ap[:, t0:t0 + TB], reads=[(DB["sp_rows"], blk)], writes=[r_])
                    for s_ in range(4):
                        gc = blk * 4 + s_
                        cs_ = slice(s_ * 128, (s_ + 1) * 128)
                        xt_, xd_, bt_, cb_, t1_ = xtok[ci % 2], xd[ci % 2], btok[ci % 2], cbm[ci % 2], t1[ci % 2]
                        bkx = B4[0]
                        P.op(PE, lambda e, o=bkx.ap.bitcast(BF16)[:, 0:128], i=x_.ap[:, cs_]: e.transpose(o, i, ident_b), reads=[x_, cstb], writes=[bkx])
                        bkb = B4[1]
                        P.op(PE, lambda e, o=bkb.ap.bitcast(BF16)[:, 0:128], i=b_.ap[:, cs_]: e.transpose(o, i, ident_b), reads=[b_, cstb], writes=[bkb])
                        bkc = B4[2]
                        P.op(PE, lambda e, o=bkc.ap[:, 0:128], w=b_.ap[:, cs_], r=c_.ap[:, cs_]: e.matmul(o, w, r, start=True, stop=True), reads=[b_, c_], writes=[bkc])
                        yield
                        for h in range(2):
                            P.op(DVE, lambda e, o=xt_.ap[:, 64 * h:64 * h + 64], i=bkx.ap.bitcast(BF16)[:, 64 * h:64 * h + 64], d=TMv[:, gc, 4 + h:5 + h]:
                                 e.tensor_scalar(out=o, in0=i, scalar1=d, scalar2=None, op0=ALU.mult), reads=[bkx, (TM, gc)], writes=[xt_])
                        P.op(ACT, lambda e, o=bt_.ap, i=bkb.ap.bitcast(BF16)[:, 0:128]: e.copy(out=o, in_=i), reads=[bkb], writes=[bt_])
                        P.op(DVE, lambda e, o=cb_.ap, i=bkc.ap[:, 0:128]: e.tensor_tensor(out=o, in0=i, in1=tri01, op=ALU.mult), reads=[bkc, cst], writes=[cb_])
                        yield
                        bky = B4[3]
                        hb_cur = Hb[hb_i % 2]
                        hs = []
                        for h in range(2):
                            k4 = (ci * 2 + h) % 4
                            bc_, sg_, M_, ec_, cc_, dc_ = bcs[k4], seg[k4], Mh[k4], ecs[k4], csc[k4], dcd[k4]
                            bkq = B4[h]
                            P.op(PE, lambda e, o=bkq.ap[:, 0:128], w=selv[0:3, h, :], r=r_.ap[0:3, cs_]: e.matmul(o, w, r, start=True, stop=True),
                                 reads=[selh, r_], writes=[bkq])
                            P.op(ACT, lambda e, o=bc_.ap, i=bkq.ap[:, 0:128]: e.copy(out=o, in_=i), reads=[bkq], writes=[bc_])
                            hs.append((bc_, sg_, M_, ec_, cc_, dc_))
                        yield
                        for h in range(2):
                            bc_, sg_, M_, ec_, cc_, dc_ = hs[h]
                            P.op(DVE, lambda e, o=sg_.ap, i=bc_.ap, n=TMv[:, gc, 1 + h:2 + h]:
                                 e.tensor_scalar(out=o, in0=i, scalar1=n, scalar2=0.0, op0=ALU.add, op1=ALU.min), reads=[bc_, (TM, gc)], writes=[sg_])
                            P.op(ACT, lambda e, o=ec_.ap, i=bc_.ap: e.activation(out=o, in_=i, func=AF.Exp), reads=[bc_], writes=[ec_])
                            P.op(ACT, lambda e, o=dc_.ap[:, 0:1], i=TMv[:, gc, 1 + h:2 + h], bia=bc_.ap[:, 127:128]:
                                 e.activation(out=o, in_=i, func=AF.Exp, bias=bia), reads=[bc_, (TM, gc)], writes=[(dc_, 0)])
                            P.op(ACT, lambda e, o=dc_.ap[:, 1:2], i=bc_.ap[:, 127:128]: e.activation(out=o, in_=i, func=AF.Exp), reads=[bc_], writes=[(dc_, 1)])
                        yield
                        for h in range(2):
                            bc_, sg_, M_, ec_, cc_, dc_ = hs[h]
                            P.op(ACT, lambda e, o=sg_.ap: e.activation(out=o, in_=o, func=AF.Exp), reads=[sg_], writes=[sg_])
                            P.op(DVE, lambda e, o=cc_.ap, a=c_.ap[:, cs_], b2=ec_.ap: e.tensor_tensor(out=o, in0=a, in1=b2, op=ALU.mult), reads=[c_, ec_], writes=[cc_])
                            P.op(DVE, lambda e, o=xd_.ap[:, 64 * h:64 * h + 64], i=xt_.ap[:, 64 * h:64 * h + 64], d=dc_.ap[:, 0:1]:
                                 e.tensor_scalar(out=o, in0=i, scalar1=d, scalar2=None, op0=ALU.mult), reads=[xt_, (dc_, 0)], writes=[xd_])
                        yield
                        for h in range(2):
                            bc_, sg_, M_, ec_, cc_, dc_ = hs[h]
                            P.op(DVE, lambda e, o=M_.ap, a=sg_.ap, b2=cb_.ap: e.tensor_tensor(out=o, in0=a, in1=b2, op=ALU.mult), reads=[sg_, cb_], writes=[M_])
                        yield
                        for h in range(2):
                            bc_, sg_, M_, ec_, cc_, dc_ = hs[h]
                            P.op(PE, lambda e, o=bky.ap[64 * h:64 * h + 64, 0:128], w=xt_.ap[:, 64 * h:64 * h + 64], r=M_.ap:
                                 e.matmul(o, w, r, start=True, stop=False), reads=[xt_, M_], writes=[bky])
                            P.op(PE, lambda e, o=bky.ap[64 * h:64 * h + 64, 0:128], w=hb_cur.ap[:, 64 * h:64 * h + 64], r=cc_.ap:
                                 e.matmul(o, w, r, start=False, stop=True), reads=[hb_cur, cc_], writes=[bky])
                        bks = B4[2]
                        P.op(PE, lambda e, o=bks.ap[:, 0:128], w=bt_.ap, r=xd_.ap: e.matmul(o, w, r, start=True, stop=True), reads=[bt_, xd_], writes=[bks])
                        yield
                        for h in range(2):
                            dc_ = hs[h][5]
                            P.op(DVE, lambda e, o=H.ap[:, 64 * h:64 * h + 64], d=dc_.ap[:, 1:2], i1=bks.ap[:, 64 * h:64 * h + 64]:
                                 e.scalar_tensor_tensor(out=o, in0=o, scalar=d, in1=i1, op0=ALU.mult, op1=ALU.add), reads=[H, (dc_, 1), bks], writes=[H])
                        hb_i += 1
                        hb_n = Hb[hb_i % 2]
                        P.op(ACT, lambda e, o=hb_n.ap, i=H.ap: e.copy(out=o, in_=i), reads=[H], writes=[hb_n])
                        P.op(DVE, lambda e, o=t1_.ap, i=x_.ap[:, cs_], d=cv(l, CV_DSK), i1=bky.ap[:, 0:128]:
                             e.scalar_tensor_tensor(out=o, in0=i, scalar=d, in1=i1, op0=ALU.mult, op1=ALU.add), reads=[x_, cvec, bky], writes=[t1_])
                        P.op(DVE, lambda e, o=y_.ap[:, cs_], a=t1_.ap, b2=z_.ap[:, cs_]: e.tensor_tensor(out=o, in0=a, in1=b2, op=ALU.mult), reads=[t1_, z_], writes=[(y_, s_)])
                        ci += 1
                        yield
                    P.dma(POOL, DB["sp_yp"].ap[:, t0:t0 + TB], y_.ap, reads=[y_], writes=[(DB["sp_yp"], blk)])
                    P.op(ACT, lambda e, o=sq.ap, i=y_.ap: e.activation(out=o, in_=i, func=AF.Square), reads=[y_], writes=[sq])
                    bk1 = B4[0]
                    P.op(PE, lambda e, o=bk1.ap[0:1, :], r=sq.ap: e.matmul(o, ones_f[:, 0:1], r, start=True, stop=True), reads=[sq, cst], writes=[bk1])
                    P.op(ACT, lambda e, o=strow.ap[0:1, :], i=bk1.ap[0:1, :]: e.copy(out=o, in_=i), reads=[bk1], writes=[strow])
                    P.dma(POOL, DB["ms_in" + sb_].ap[0:1, lt0:lt0 + TB], strow.ap[0:1, :], reads=[strow], writes=[(DB["ms_in" + sb_], (0, blk))])
                    yield

            gens = [stream(0, banks[0:4]), stream(1, banks[4:8])]
            for g_ in gens:
                next(g_)
            alive = list(gens)
            while alive:
                for g_ in list(alive):
                    try:
                        next(g_)
                    except StopIteration:
                        alive.remove(g_)
            ag("ms_in", "ms_out", 0)
            ag("ms_in", "ms_out", 1)
            A.release(mk)

        def stage_finalize(l):
            mk = A.mark()
            ypb = [A.alloc(f"f_yp{i}", TB * 4) for i in range(2)]
            cfb = [A.alloc(f"f_cf{i}", TB * 4) for i in range(2)]
            gdb = [A.alloc(f"f_gd{i}", TB * 2, BF16) for i in range(2)]
            stb = [A.alloc(f"f_st{i}", TB * 4) for i in range(2)]
            rb2 = [A.alloc(f"f_rb{i}", TB * 4) for i in range(2)]
            mu2 = [A.alloc(f"f_mu{i}", TB * 4) for i in range(2)]
            va2 = [A.alloc(f"f_va{i}", TB * 4) for i in range(2)]
            tt2 = [A.alloc(f"f_tt{i}", TB * 4) for i in range(2)]
            ob = [A.alloc(f"f_ob{i}", TB * 2, BF16) for i in range(4)]
            sel = A.alloc("f_sel", 3 * 128 * 4)
            selv = sel.ap.rearrange("p (j m) -> p j m", j=3)
            for j in range(3):
                P.op(DVE, lambda e, o=selv[0:24, j, :], j=j: e.tensor_scalar(out=o, in0=ones_f[0:24, :], scalar1=sel3_f[0:24, j:j + 1], scalar2=None, op0=ALU.mult),
                     reads=[cst, sel3], writes=[sel])
            oi = 0
            for blk in range(NBLK):
                t0 = blk * TB
                lt0 = (blk % BPS) * TB
                sb_ = str(blk // BPS)
                y_, c_, g_, s_ = ypb[blk % 2], cfb[blk % 2], gdb[blk % 2], stb[blk % 2]
                rb, mu, va, tt = rb2[blk % 2], mu2[blk % 2], va2[blk % 2], tt2[blk % 2]
                P.dma(SP, y_.ap, DB["sp_yp"].ap[:, t0:t0 + TB], reads=[(DB["sp_yp"], blk)], writes=[y_])
                P.dma(SP, c_.ap, DB["sp_cf"].ap[:, t0:t0 + TB], reads=[(DB["sp_cf"], blk)], writes=[c_])
                P.dma(SP, g_.ap, DB["sp_gd"].ap[:, t0:t0 + TB], reads=[(DB["sp_gd"], blk)], writes=[g_])
                P.dma(SP, s_.ap[0:24, :], DB["ms_out" + sb_].ap[:, lt0:lt0 + TB], reads=[DB["ms_out" + sb_]], writes=[s_])
                bks = []
                for j in range(3):
                    bk = bank()
                    P.op(PE, lambda e, o=bk.ap, w=selv[0:24, j, :], r=s_.ap[0:24, :]: e.matmul(o, w, r, start=True, stop=True), reads=[sel, s_], writes=[bk])
                    bks.append(bk)
                P.op(ACT, lambda e, o=rb.ap, i=bks[0].ap: e.activation(out=o, in_=i, func=AF.Sqrt, bias=EPS, scale=1.0 / 1024), reads=[bks[0]], writes=[rb])
                P.op(DVE, lambda e, o=rb.ap: e.reciprocal(out=o, in_=o), reads=[rb], writes=[rb])
                o1 = ob[oi % 4]
                oi += 1
                P.op(DVE, lambda e, o=o1.ap, i=y_.ap, w=cv(l, CV_SNW), r=rb.ap: e.scalar_tensor_tensor(out=o, in0=i, scalar=w, in1=r, op0=ALU.mult, op1=ALU.mult),
                     reads=[y_, cvec, rb], writes=[o1])
                P.dma(POOL, DB["y_in" + sb_].ap[128:256, lt0:lt0 + TB], o1.ap, reads=[o1], writes=[(DB["y_in" + sb_], (1, blk))])
                P.op(ACT, lambda e, o=mu.ap, i=bks[1].ap: e.activation(out=o, in_=i, func=AF.Copy, scale=1.0 / 1024), reads=[bks[1]], writes=[mu])
                P.op(DVE, lambda e, o=va.ap, a=mu.ap: e.tensor_tensor(out=o, in0=a, in1=a, op=ALU.mult), reads=[mu], writes=[va])
                P.op(DVE, lambda e, o=va.ap, i=bks[2].ap: e.scalar_tensor_tensor(out=o, in0=i, scalar=1.0 / 1024, in1=o, op0=ALU.mult, op1=ALU.subtract),
                     reads=[bks[2], va], writes=[va])
                P.op(ACT, lambda e, o=va.ap: e.activation(out=o, in_=o, func=AF.Sqrt, bias=EPS), reads=[va], writes=[va])
                P.op(DVE, lambda e, o=va.ap: e.reciprocal(out=o, in_=o), reads=[va], writes=[va])
                P.op(DVE, lambda e, o=tt.ap, a=c_.ap, b2=mu.ap: e.tensor_tensor(out=o, in0=a, in1=b2, op=ALU.subtract), reads=[c_, mu], writes=[tt])
                P.op(DVE, lambda e, o=tt.ap, b2=va.ap: e.tensor_tensor(out=o, in0=o, in1=b2, op=ALU.mult), reads=[tt, va], writes=[tt])
                P.op(ACT, lambda e, o=tt.ap: e.activation(out=o, in_=o, func=AF.Silu, bias=cv(l, CV_LNB), scale=cv(l, CV_LNW)), reads=[tt, cvec], writes=[tt])
                o2 = ob[oi % 4]
                oi += 1
                P.op(DVE, lambda e, o=o2.ap, a=tt.ap, b2=g_.ap: e.tensor_tensor(out=o, in0=a, in1=b2, op=ALU.mult), reads=[tt, g_], writes=[o2])
                P.dma(POOL, DB["y_in" + sb_].ap[384:512, lt0:lt0 + TB], o2.ap, reads=[o2], writes=[(DB["y_in" + sb_], (3, blk))])
                ag_if_last("y_in", "y_out", blk)
            A.release(mk)

        def load_wm(l, Wg, Wb, Wo):
            gsrc = wg_in[l].rearrange("b (k p) n -> p b k n", p=128)
            gdst = Wg.ap.rearrange("p (b k n) -> p b k n", b=4, k=KT)
            for b in range(4):
                for k0 in range(0, KT, 4):
                    P.dma(POOL, gdst[:, b, k0:k0 + 4, :], gsrc[:, b, k0:k0 + 4, :], writes=[(Wg, (b, k0))])
            bsrc = wb_in[l].rearrange("b (k p) n -> p b k n", p=128)
            bdst = Wb.ap.rearrange("p (b k n) -> p b k n", b=4, k=8)
            for b in range(4):
                P.dma(POOL, bdst[:, b, :, :], bsrc[:, b, :, :], writes=[(Wb, b)])
            osrc = wo_in[l].rearrange("(k p) n -> p k n", p=128)
            odst = Wo.ap.rearrange("p (k n) -> p k n", k=KT)
            for k0 in range(0, KT, 4):
                P.dma(POOL, odst[:, k0:k0 + 4, :], osrc[:, k0:k0 + 4, :], writes=[(Wo, k0)])

        def stage_merge(l, Wg, Wb):
            mk = A.mark()
            hT = [A.alloc(f"m_h{i}", KT * TB * 2, BF16) for i in range(2)]
            yT = [A.alloc(f"m_y{i}", 8 * TB * 2, BF16) for i in range(3)]
            gt = [A.alloc(f"m_g{i}", TB * 4) for i in range(2)]
            tm = [A.alloc(f"m_t{i}", TB * 4) for i in range(2)]
            ma = [A.alloc(f"m_a{i}", TB * 4) for i in range(4)]
            mo = [A.alloc(f"m_o{i}", TB * 2, BF16) for i in range(4)]
            Wgv = Wg.ap.rearrange("p (b k n) -> p b k n", b=4, k=KT)
            Wbv = Wb.ap.rearrange("p (b k n) -> p b k n", b=4, k=8)
            gi = 0
            yi = 0
            for blk in range(NBLK):
                t0 = blk * TB
                lt0 = (blk % BPS) * TB
                sb_ = str(blk // BPS)
                h_ = hT[blk % 2]
                hv = h_.ap.rearrange("p (k t) -> p k t", k=KT)
                P.dma(SP, hv, DB["h_out" + sb_].ap.rearrange("(k p) t -> p k t", p=128)[:, :, lt0:lt0 + TB], reads=[DB["h_out" + sb_]], writes=[h_])
                accs = [ma[(blk * 2 + ct) % 4] for ct in range(2)]
                ysrc = DB["y_out" + sb_].ap.rearrange("(r b p) t -> p b r t", r=8, b=4)
                for b in range(4):
                    y_ = yT[yi % 3]
                    yi += 1
                    yv = y_.ap.rearrange("p (k t) -> p k t", k=8)
                    P.dma(SP, yv, ysrc[:, b, :, lt0:lt0 + TB], reads=[DB["y_out" + sb_]], writes=[y_])
                    for ct in range(2):
                        a_ = accs[ct]
                        bg = bank()
                        for k in range(KT):
                            P.op(PE, lambda e, o=bg.ap, w=Wgv[:, b, k, ct * 128:(ct + 1) * 128], r=hv[:, k, :], k=k:
                                 e.matmul(o, w, r, start=(k == 0), stop=(k == KT - 1)), reads=[h_, (Wg, (b, k // 4 * 4))], writes=[bg])
                        g_ = gt[gi % 2]
                        P.op(ACT, lambda e, o=g_.ap, i=bg.ap, bia=cv(l, CV_BG + b * 2 + ct): e.activation(out=o, in_=i, func=AF.Sigmoid, bias=bia),
                             reads=[bg, cvec], writes=[g_])
                        bb = bank()
                        for k in range(8):
                            P.op(PE, lambda e, o=bb.ap, w=Wbv[:, b, k, ct * 128:(ct + 1) * 128], r=yv[:, k, :], k=k:
                                 e.matmul(o, w, r, start=(k == 0), stop=(k == 7)), reads=[y_, (Wb, b)], writes=[bb])
                        if b == 0:
                            P.op(DVE, lambda e, o=a_.ap, a=bb.ap, g=g_.ap: e.tensor_tensor(out=o, in0=a, in1=g, op=ALU.mult), reads=[bb, g_], writes=[a_])
                        else:
                            t_ = tm[gi % 2]
                            P.op(DVE, lambda e, o=t_.ap, a=bb.ap, g=g_.ap: e.tensor_tensor(out=o, in0=a, in1=g, op=ALU.mult), reads=[bb, g_], writes=[t_])
                            P.op(DVE, lambda e, o=a_.ap, t=t_.ap: e.tensor_tensor(out=o, in0=o, in1=t, op=ALU.add), reads=[a_, t_], writes=[a_])
                        gi += 1
                for ct in range(2):
                    a_ = accs[ct]
                    o_ = mo[(blk * 2 + ct) % 4]
                    P.op(ACT, lambda e, o=o_.ap, i=a_.ap: e.copy(out=o, in_=i), reads=[a_], writes=[o_])
                    P.dma(POOL, DB["m_in" + sb_].ap[ct * 128:(ct + 1) * 128, lt0:lt0 + TB], o_.ap, reads=[o_], writes=[(DB["m_in" + sb_], (ct, blk))])
                ag_if_last("m_in", "m_out", blk)
            A.release(mk)

        def stage_out(l, Wo, xsrc_buf, xsrc_ap):
            mk = A.mark()
            mT = [A.alloc(f"o_m{i}", KT * TB * 2, BF16) for i in range(2)]
            xb = [A.alloc(f"o_x{i}", 2 * TB * 4) for i in range(2)]
            xn = [A.alloc(f"o_n{i}", 2 * TB * 4) for i in range(2)]
            sq = [A.alloc(f"o_q{i}", 2 * TB * 4) for i in range(2)]
            row = [A.alloc(f"o_r{i}", TB * 4) for i in range(2)]
            Wov = Wo.ap.rearrange("p (k n) -> p k n", k=KT)
            for blk in range(NBLK):
                t0 = blk * TB
                lt0 = (blk % BPS) * TB
                sb_ = str(blk // BPS)
                m_, x_, n_, q_, r_ = mT[blk % 2], xb[blk % 2], xn[blk % 2], sq[blk % 2], row[blk % 2]
                mv = m_.ap.rearrange("p (k t) -> p k t", k=KT)
                xv = x_.ap.rearrange("p (c t) -> p c t", c=2)
                nv = n_.ap.rearrange("p (c t) -> p c t", c=2)
                qv = q_.ap.rearrange("p (c t) -> p c t", c=2)
                P.dma(SP, mv, DB["m_out" + sb_].ap.rearrange("(k p) t -> p k t", p=128)[:, :, lt0:lt0 + TB], reads=[DB["m_out" + sb_]], writes=[m_])
                P.dma(SP, xv, xsrc_ap.rearrange("(c p) t -> p c t", p=128)[:, :, t0:t0 + TB], reads=[(xsrc_buf, blk)], writes=[x_])
                for ct in range(2):
                    bk = bank()
                    for k in range(KT):
                        P.op(PE, lambda e, o=bk.ap, w=Wov[:, k, ct * 128:(ct + 1) * 128], r=mv[:, k, :], k=k:
                             e.matmul(o, w, r, start=(k == 0), stop=(k == KT - 1)), reads=[m_, (Wo, k // 4 * 4)], writes=[bk])
                    P.op(DVE, lambda e, o=nv[:, ct, :], a=bk.ap, b2=xv[:, ct, :]: e.tensor_tensor(out=o, in0=a, in1=b2, op=ALU.add), reads=[bk, x_], writes=[(n_, ct)])
                P.dma(POOL, DB["xres"].ap.rearrange("(c p) t -> p c t", p=128)[:, :, t0:t0 + TB], nv, reads=[n_], writes=[(DB["xres"], blk)])
                P.op(ACT, lambda e, a=q_.ap, b2=n_.ap: e.activation(out=a, in_=b2, func=AF.Square), reads=[n_], writes=[q_])
                bk = bank()
                for c in range(2):
                    P.op(PE, lambda e, o=bk.ap[0:1, :], r=qv[:, c, :], c=c: e.matmul(o, ones_f[:, 0:1], r, start=(c == 0), stop=(c == 1)), reads=[q_, cst], writes=[bk])
                P.op(ACT, lambda e, o=r_.ap[0:1, :], i=bk.ap[0:1, :]: e.copy(out=o, in_=i), reads=[bk], writes=[r_])
                P.dma(POOL, DB["rs_in" + sb_].ap[0:1, lt0:lt0 + TB], r_.ap[0:1, :], reads=[r_], writes=[(DB["rs_in" + sb_], blk)])
                ag_if_last("rs_in", "rs_out", blk)
            A.release(mk)

        sel3 = A.alloc("sel3", 4 * 4)
        sel3_f = sel3.ap
        sel3_in = din("sel3", [24, 3])
        P.dma(SP, sel3.ap[0:24, 0:3], sel3_in, writes=[sel3])
        W1 = A.alloc("W1", KT * NC1 * 2 + 4, BF16)
        W1.ap = W1.ap[:, 0:KT * NC1]

        xin_buf = Buf("xT_in", xT_in)
        load_w1(0, W1)
        stage_rms_stats_from(xin_buf, xT_in)
        for l in range(NL):
            xb_, xa_ = (xin_buf, xT_in) if l == 0 else (DB["xres"], DB["xres"].ap)
            stage_norm(l, xb_, xa_)
            stage_inproj(l, W1)
            if l + 1 < NL:
                load_w1(l + 1, W1)
            mkw = A.mark()
            Wg = A.alloc("Wg", 4 * KT * 256 * 2, BF16)
            Wb = A.alloc("Wb", 4 * 8 * 256 * 2, BF16)
            Wo = A.alloc("Wo", KT * 256 * 2, BF16)
            load_wm(l, Wg, Wb, Wo)
            stage_attention()
            stage_ssd(l)
            stage_finalize(l)
            stage_merge(l, Wg, Wb)
            stage_out(l, Wo, xb_, xa_)
            A.release(mkw)
        stage_norm(0, DB["xres"], DB["xres"].ap, final=True)

        tap_ops = []
        for (nm, shp, dt) in taps:
            src = DB[nm]
            rows = shp[0]
            step = max(1, rows // 8)
            for r0 in range(0, rows, step):
                tap_ops.append(P.dma(POOL, tap_out[nm][r0:r0 + step, :], src.ap[r0:r0 + step, :], reads=[src], writes=[(DB["tap_" + nm], r0)]))
        out_ops = [op for op in P.ops["pool"] if op.kind == "d"][-(NBLK + 8):] + tap_ops
        P.final_wait(POOL, out_ops)

        with nc.Block() as block:
            P.emit(block, sems)
    return nc


def host_consts():
    c = np.zeros((128, 1024), np.float32)
    c[:, 0:128] = np.eye(128, dtype=np.float32)
    s = np.arange(128)[:, None]
    l = np.arange(128)[None, :]
    c[:, 128:256] = (s <= l).astype(np.float32)
    c[:, 256:384] = np.where(s > l, -30000.0, 0.0).astype(np.float32)
    c[:, 384:512] = 1.0
    cm = np.ones((3, 512), np.float32)
    cm[1:3, 0::128] = 0.0
    c[0:3, 512:1024] = cm
    sel3 = np.zeros((24, 3), np.float32)
    for r in range(24):
        sel3[r, r % 3] = 1.0
    return c, sel3


def core_inputs(c, inp, NL):
    f32 = np.float32
    x = inp["x"].reshape(T, D)
    d0 = 256 * c
    out = {}
    out["xT"] = np.ascontiguousarray(x[:, d0:d0 + 256].T)
    hc = slice(128 * c, 128 * c + 128)
    g = c // 4
    cols = np.concatenate([
        OFF_Q + np.arange(128 * c, 128 * c + 128), OFF_K + np.arange(128 * c, 128 * c + 128),
        OFF_GA + np.arange(128 * c, 128 * c + 128), OFF_Z + np.arange(128 * c, 128 * c + 128),
        OFF_XBC + np.arange(128 * c, 128 * c + 128),
        OFF_XBC + 1024 + np.arange(128 * g, 128 * g + 128),
        OFF_XBC + 1280 + np.arange(128 * g, 128 * g + 128),
        OFF_SCB + np.arange(128 * c, 128 * c + 128), OFF_SCC + np.arange(128 * c, 128 * c + 128),
        OFF_SCX + np.arange(128 * c, 128 * c + 128), OFF_GC + np.arange(128 * c, 128 * c + 128),
        OFF_GLUA + np.arange(128 * c, 128 * c + 128), OFF_GLUA + 1024 + np.arange(128 * c, 128 * c + 128),
        OFF_GD + np.arange(128 * c, 128 * c + 128),
        OFF_V + np.arange(128 * c, 128 * c + 128),
        np.array([OFF_F + c, OFF_DT + 2 * c, OFF_DT + 2 * c + 1]),
    ])
    assert cols.size == NC1
    out["w1"] = np.ascontiguousarray(inp["w_in"][:NL][:, :, cols])
    out["wg"] = np.ascontiguousarray(inp["w_gate"][:NL][:, :, :, d0:d0 + 256])
    out["wb"] = np.ascontiguousarray(inp["w_branch"][:NL][:, :, :, d0:d0 + 256])
    out["wo"] = np.ascontiguousarray(inp["w_out"][:NL][:, :, d0:d0 + 256])
    cv = np.zeros((128, NL, NV), f32)
    for l in range(NL):
        cv[:, l, CV_NW] = inp["norm_w"][l, d0:d0 + 128]
        cv[:, l, CV_NW + 1] = inp["norm_w"][l, d0 + 128:d0 + 256]
        scw = inp["ssm_conv_w"][l]
        scb = inp["ssm_conv_b"][l]
        for j, ch in enumerate((np.arange(128 * c, 128 * c + 128), 1024 + np.arange(128 * g, 128 * g + 128),
                                1280 + np.arange(128 * g, 128 * g + 128))):
            cv[:, l, CV_SSMW + 4 * j:CV_SSMW + 4 * j + 4] = scw[:, ch].T
            cv[:, l, CV_SSMB + j] = scb[ch]
        cv[:, l, CV_SCW:CV_SCW + 3] = inp["sc_conv_w"][l][:, hc].T
        cv[:, l, CV_SCB] = inp["sc_conv_b"][l][hc]
        cv[:, l, CV_CFW:CV_CFW + 31] = inp["cf_conv_w"][l][:, hc].T
        cv[:, l, CV_CFB] = inp["cf_conv_b"][l][hc]
        cv[:, l, CV_LNW] = inp["cf_ln_w"][l][hc]
        cv[:, l, CV_LNB] = inp["cf_ln_b"][l][hc]
        cv[:, l, CV_SNW] = inp["ssm_norm_w"][l][hc]
        cv[0:64, l, CV_DSK] = inp["d_skip"][l][2 * c]
        cv[64:128, l, CV_DSK] = inp["d_skip"][l][2 * c + 1]
        for b in range(4):
            for ct in range(2):
                cv[:, l, CV_BG + 2 * b + ct] = inp["b_gate"][l, b, d0 + 128 * ct:d0 + 128 * ct + 128]
        cv[0, l, CV_SSC] = -1.0
        cv[1:3, l, CV_SSC] = 1.0
        cv[0, l, CV_SBI] = inp["fg_bias"][l, c]
        cv[1, l, CV_SBI] = inp["dt_bias"][l, 2 * c]
        cv[2, l, CV_SBI] = inp["dt_bias"][l, 2 * c + 1]
        cv[0, l, CV_ALOG] = 0.0
        cv[1, l, CV_ALOG] = inp["a_log"][l, 2 * c]
        cv[2, l, CV_ALOG] = inp["a_log"][l, 2 * c + 1]
        cv[:, l, CV_FNW] = inp["final_norm_w"][d0:d0 + 128]
        cv[:, l, CV_FNW + 1] = inp["final_norm_w"][d0 + 128:d0 + 256]
    out["cvec"] = np.ascontiguousarray(cv.reshape(128, NL * NV))
    cst, sel3 = host_consts()
    out["cst"] = cst
    out["sel3"] = sel3
    return out


_NC_CACHE = {}


def run(inputs, NL=L_FULL, taps=()):
    import time, sys
    t0 = time.time()
    inp = {k: np.asarray(v) for k, v in inputs.items()}
    key = (NL, tuple(taps))
    if key not in _NC_CACHE:
        _NC_CACHE[key] = build_program(NL, taps)
    nc = _NC_CACHE[key]
    t1 = time.time()
    in_maps = [core_inputs(c, inp, NL) for c in range(NCORES)]
    t2 = time.time()
    res = run_bass_kernel_spmd(nc, in_maps, core_ids=list(range(NCORES)))
    t3 = time.time()
    print(f"[kernel] build {t1 - t0:.1f}s prep {t2 - t1:.1f}s run {t3 - t2:.1f}s", file=sys.stderr, flush=True)
    return res


def kernel(**inputs):
    res = run(inputs)
    outT = np.concatenate([res.results[c]["outT"] for c in range(NCORES)], axis=0)
    return np.ascontiguousarray(outT.T).reshape(2, SEQ, D).astype(np.float32)
```
